# Optimizing a Trainium2 kernel written in Bass

```python
import math
import jax, jax.numpy as jnp
from jax import lax
import numpy as np

D_MODEL = 1024
BATCH = 4
SEQ = 4096
DEPTH = 4

D_CONV = D_MODEL
CONV_GROUPS = 8
SHORT_CONV_K = 3
D_RNN = D_MODEL
RNN_HEADS = 4
RNN_HEAD_DIM = D_RNN // RNN_HEADS
RNN_CONV_K = 4
LRU_C = 8.0
N_BRANCH = 2
D_FF = 2816
D_IN = 3 * D_CONV + 2 * D_RNN + N_BRANCH * D_MODEL
SPLITS = (D_CONV, 2 * D_CONV, 3 * D_CONV, 3 * D_CONV + D_RNN, 3 * D_CONV + 2 * D_RNN)
ALPHA = (2.0 * DEPTH) ** 0.25
BETA = (8.0 * DEPTH) ** -0.25
LN_EPS = 1e-5

kernel_name = "hybrid_shortconv_rglru_macaron_deepnorm"


def layer_norm(x, g, b):
    xf = x.astype(jnp.float32)
    mu = jnp.mean(xf, axis=-1, keepdims=True)
    var = jnp.mean(jnp.square(xf - mu), axis=-1, keepdims=True)
    y = (xf - mu) * lax.rsqrt(var + LN_EPS)
    return (y * g.astype(jnp.float32) + b.astype(jnp.float32)).astype(x.dtype)


def causal_depthwise_conv(x, w, k):
    s = x.shape[1]
    xp = jnp.pad(x, ((0, 0), (k - 1, 0), (0, 0)))
    out = xp[:, 0:s, :] * w[0]
    for i in range(1, k):
        out = out + xp[:, i:i + s, :] * w[i]
    return out


def swiglu(x, w1, w2):
    g, u = jnp.split(x @ w1, 2, axis=-1)
    return (jax.nn.silu(g) * u) @ w2


def rg_lru(x, gate_in, gate_rec, a_param):
    s = x.shape[1]
    log_a = -LRU_C * gate_rec * jax.nn.softplus(-a_param)
    a = jnp.exp(log_a)
    mult = jnp.sqrt(-jnp.expm1(2.0 * log_a))
    is_start = (jnp.arange(s) == 0)[None, :, None]
    mult = jnp.where(is_start, 1.0, mult)
    b = x * gate_in * mult

    def combine(left, right):
        a1, b1 = left
        a2, b2 = right
        return a1 * a2, a2 * b1 + b2

    _, h = lax.associative_scan(combine, (a, b), axis=1)
    return h


def hybrid_mixer(x, w_in, b_merge, sc_w, rc_w, rc_b, rg_w, rg_b, a_param,
                 w_out_conv, w_out_rnn, w_o):
    bsz, s, _ = x.shape
    proj = x @ w_in
    b_gate, c_gate, v, x_r, y_r, g_logits = jnp.split(proj, SPLITS, axis=-1)

    y_a = (b_gate * causal_depthwise_conv(c_gate * v, sc_w, SHORT_CONV_K)) @ w_out_conv

    xc = causal_depthwise_conv(x_r, rc_w, RNN_CONV_K) + rc_b
    xh = xc.reshape(bsz, s, RNN_HEADS, RNN_HEAD_DIM)
    gates = jnp.einsum('bshi,ghij->gbshj', xh, rg_w).reshape(2, bsz, s, D_RNN)
    gates = jax.nn.sigmoid((gates + rg_b[:, None, None, :]).astype(jnp.float32))
    h = rg_lru(xc.astype(jnp.float32), gates[0], gates[1],
               a_param.astype(jnp.float32)).astype(x.dtype)
    y_b = (h * jax.nn.gelu(y_r)) @ w_out_rnn

    g_a, g_b = jnp.split(jax.nn.sigmoid(g_logits + b_merge), 2, axis=-1)
    return (g_a * y_a + g_b * y_b) @ w_o


def setup_inputs(seed: int = 0) -> dict:
    key = jax.random.key(seed)
    ks = jax.random.split(key, 20)
    f32 = jnp.float32
    nrm = lambda k, shape, scale: jax.random.normal(k, shape, f32) * scale

    x = jax.random.normal(ks[0], (BATCH, SEQ, D_MODEL), f32)
    w_in = nrm(ks[1], (DEPTH, D_MODEL, D_IN), D_MODEL ** -0.5)
    b_merge = nrm(ks[2], (DEPTH, N_BRANCH * D_MODEL), 0.1)
    sc_w = nrm(ks[3], (DEPTH, SHORT_CONV_K, D_CONV), SHORT_CONV_K ** -0.5)
    rc_w = nrm(ks[4], (DEPTH, RNN_CONV_K, D_RNN), RNN_CONV_K ** -0.5)
    rc_b = nrm(ks[5], (DEPTH, D_RNN), 0.02)
    rg_w = nrm(ks[6], (DEPTH, 2, RNN_HEADS, RNN_HEAD_DIM, RNN_HEAD_DIM), RNN_HEAD_DIM ** -0.5)
    rg_b = nrm(ks[7], (DEPTH, 2, D_RNN), 0.1)
    u = jax.random.uniform(ks[8], (DEPTH, D_RNN), f32, 0.9, 0.999)
    s_a = u ** (1.0 / LRU_C)
    a_param = jnp.log(s_a) - jnp.log1p(-s_a)
    w_out_conv = nrm(ks[9], (DEPTH, D_CONV, D_MODEL), D_CONV ** -0.5)
    w_out_rnn = nrm(ks[10], (DEPTH, D_RNN, D_MODEL), D_RNN ** -0.5)
    w_o = nrm(ks[11], (DEPTH, D_MODEL, D_MODEL), BETA * D_MODEL ** -0.5)
    ffn_w1 = nrm(ks[12], (DEPTH, 2, D_MODEL, 2 * D_FF), D_MODEL ** -0.5)
    ffn_w2 = nrm(ks[13], (DEPTH, 2, D_FF, D_MODEL), BETA * D_FF ** -0.5)
    ln_g = 1.0 + nrm(ks[14], (DEPTH, 3, D_MODEL), 0.02)
    ln_b = nrm(ks[15], (DEPTH, 3, D_MODEL), 0.02)
    return {"x": x, "w_in": w_in, "b_merge": b_merge, "sc_w": sc_w, "rc_w": rc_w,
            "rc_b": rc_b, "rg_w": rg_w, "rg_b": rg_b, "a_param": a_param,
            "w_out_conv": w_out_conv, "w_out_rnn": w_out_rnn, "w_o": w_o,
            "ffn_w1": ffn_w1, "ffn_w2": ffn_w2, "ln_g": ln_g, "ln_b": ln_b}


def reference(x, w_in, b_merge, sc_w, rc_w, rc_b, rg_w, rg_b, a_param,
              w_out_conv, w_out_rnn, w_o, ffn_w1, ffn_w2, ln_g, ln_b):
    for l in range(DEPTH):
        x = layer_norm(ALPHA * x + 0.5 * swiglu(x, ffn_w1[l, 0], ffn_w2[l, 0]),
                       ln_g[l, 0], ln_b[l, 0])
        mix = hybrid_mixer(x, w_in[l], b_merge[l], sc_w[l], rc_w[l], rc_b[l], rg_w[l],
                           rg_b[l], a_param[l], w_out_conv[l], w_out_rnn[l], w_o[l])
        x = layer_norm(ALPHA * x + mix, ln_g[l, 1], ln_b[l, 1])
        x = layer_norm(ALPHA * x + 0.5 * swiglu(x, ffn_w1[l, 1], ffn_w2[l, 1]),
                       ln_g[l, 2], ln_b[l, 2])
    return x
```

```python
import numpy as np
from contextlib import ExitStack
import concourse.bass as bass
import concourse.mybir as mybir
from concourse.bass_utils import run_bass_kernel_spmd

AF = mybir.ActivationFunctionType
ALU = mybir.AluOpType
AX = mybir.AxisListType
F32 = mybir.dt.float32
BF16 = mybir.dt.bfloat16

P = 128
D = 1024
KC = 8
TOK = 2048
NT = 512
NTL = TOK // NT
FC = 22
DEPTH = 4
DFF = 2816
ALPHA = 8.0 ** 0.25
EPS_P = 1e-5 / (ALPHA * ALPHA)
RS = 0.5 / ALPHA
NBLK = 216
BLKW = 1024
NPRM = 152
O_LNG, O_LNB, O_BM, O_SCW, O_RCW, O_RCB, O_RGB, O_AP = 0, 24, 48, 64, 88, 120, 128, 144
NDER = 48
NS = 4
NB = 16
LAG = 2
G = 2
CW = 48
NSCR = 12
SCW = 516


class Eng:
    def __init__(self, name, h, sem, inc=1, inorder=False):
        self.name, self.h, self.sem, self.inc = name, h, sem, inc
        self.cnt = 0
        self.waited = {}
        self.inorder = inorder


class Buf:
    def __init__(self, name):
        self.name = name
        self.w = None
        self.r = {}


class Prog:
    def __init__(self, nl, ncores=8):
        self.nl = nl
        self.ncores = ncores
        self.nc = bass.Bass("TRN2", target_bir_lowering=False)
        self.es = ExitStack()

    def _deps(self, reads, writes):
        deps = {}
        for b in reads:
            if b.w is not None:
                e, c = b.w
                deps[e] = max(deps.get(e, 0), c)
        for b in writes:
            if b.w is not None:
                e, c = b.w
                deps[e] = max(deps.get(e, 0), c)
            for e, c in b.r.items():
                deps[e] = max(deps.get(e, 0), c)
        return deps

    def _wait(self, eng, deps):
        for e, c in deps.items():
            if e is eng and eng.inorder:
                continue
            if eng.waited.get(e.name, 0) >= c:
                continue
            eng.h.wait_ge(e.sem, c)
            eng.waited[e.name] = c

    def _record(self, prod, reads, writes):
        for b in writes:
            b.w = (prod, prod.cnt)
            b.r = {}
        for b in reads:
            b.r[prod] = prod.cnt

    def emit(self, eng, fn, reads=(), writes=()):
        self._wait(eng, self._deps(reads, writes))
        ins = fn()
        eng.cnt += eng.inc
        ins.then_inc(eng.sem, eng.inc)
        self._record(eng, reads, writes)
        return ins

    def dma(self, out, in_, reads, writes, dsem_eng):
        self._wait(self.SP, self._deps(reads, writes))
        ins = self.nc.sync.dma_start(out=out, in_=in_)
        dsem_eng.cnt += 16
        ins.then_inc(dsem_eng.sem, 16)
        self._record(dsem_eng, reads, writes)
        return ins

    def new_dsem(self, name):
        sem = self.es.enter_context(self.nc.semaphore(name))
        return Eng(name, None, sem, inc=16)

    def act(self, out, in_, func, reads, writes, bias=None, scale=None):
        kw = {}
        if bias is not None:
            kw["bias"] = bias
        if scale is not None:
            kw["scale"] = scale
        return self.emit(self.ACT, lambda: self.nc.scalar.activation(out=out, in_=in_, func=func, **kw),
                         reads, writes)

    def tt(self, out, in0, in1, op, reads, writes):
        return self.emit(self.DVE, lambda: self.nc.vector.tensor_tensor(out=out, in0=in0, in1=in1, op=op),
                         reads, writes)

    def stt(self, out, in0, scalar, in1, op0, op1, reads, writes):
        return self.emit(self.DVE, lambda: self.nc.vector.scalar_tensor_tensor(
            out=out, in0=in0, scalar=scalar, in1=in1, op0=op0, op1=op1), reads, writes)

    def ts(self, out, in0, s1, s2, op0, op1, reads, writes):
        return self.emit(self.DVE, lambda: self.nc.vector.tensor_scalar(
            out=out, in0=in0, scalar1=s1, scalar2=s2, op0=op0, op1=op1), reads, writes)

    def tsm(self, out, in0, s1, reads, writes):
        return self.emit(self.DVE, lambda: self.nc.vector.tensor_scalar_mul(out=out, in0=in0, scalar1=s1),
                         reads, writes)

    def cp(self, out, in_, reads, writes):
        return self.emit(self.DVE, lambda: self.nc.vector.tensor_copy(out=out, in_=in_), reads, writes)

    def mm(self, bank, pairs, reads):
        pt = self.ps[bank]
        n = len(pairs)

        def fn():
            ins = None
            for i, (l, r) in enumerate(pairs):
                ins = self.nc.tensor.matmul(pt[:], lhsT=l, rhs=r, start=(i == 0), stop=(i == n - 1))
            return ins
        return self.emit(self.PE, fn, reads, [self.psb[bank]])

    def bankA(self):
        b = self._ra
        self._ra = (self._ra + 1) % 4
        return b

    def bankB(self):
        b = 4 + self._rb
        self._rb = (self._rb + 1) % 4
        return b

    def bank8(self):
        b = self._r8
        self._r8 = (self._r8 + 1) % 8
        return b

    def pump(self):
        nblk = len(self.seq)
        progress = True
        while progress:
            progress = False
            if self.dptr < nblk and self.dptr - self.cptr < NS:
                k = self.dptr
                l, bi = self.seq[k]
                s = k % NS
                self.dma(self.stg[:, s, :], self.wst[l, bi], [], [self.stg_b[s]], self.stg_d[s])
                self.dptr += 1
                progress = True
            lagok = (self.cptr < self.dptr - LAG) or (self.dptr == nblk and self.cptr < self.dptr)
            if lagok and self.cptr - self.rel < NB:
                k = self.cptr
                s = k % NS
                r = k % NB
                self.act(self.wr[:, r, :], self.stg[:, s, :], AF.Copy, [self.stg_b[s]], [self.wr_b[r]])
                self.cptr += 1
                progress = True

    def wget(self, l, bi):
        k = self.gptr
        assert self.seq[k] == (l, bi), (k, self.seq[k], (l, bi))
        while self.cptr <= k:
            before = (self.dptr, self.cptr)
            self.pump()
            if self.cptr <= k and (self.dptr, self.cptr) == before:
                kk = self.cptr
                assert kk < self.dptr and kk - self.rel < NB, (kk, self.dptr, self.rel)
                self.act(self.wr[:, kk % NB, :], self.stg[:, kk % NS, :], AF.Copy,
                         [self.stg_b[kk % NS]], [self.wr_b[kk % NB]])
                self.cptr += 1
        self.gptr += 1
        r = k % NB
        return self.wr[:, r, :], self.wr_b[r]

    def release(self, n):
        self.rel += n
        assert self.rel <= self.gptr
        self.pump()

    def layer_seq(self, l):
        s = []
        for f in range(2):
            if f == 1:
                for t in range(NTL):
                    for h in range(4):
                        for jj in range(2):
                            j = 2 * h + jj
                            s += [(l, 132 + 16 + j), (l, 132 + 8 + j), (l, 132 + j),
                                  (l, 132 + 24 + j), (l, 132 + 32 + j)]
                        s.append((l, 188 + h))
                    for m in range(8):
                        s += [(l, 132 + 40 + m), (l, 132 + 48 + m), (l, 192 + m), (l, 200 + m)]
                    for m in range(8):
                        s.append((l, 208 + m))
            for j in range(FC):
                s += [(l, 66 * f + 3 * j), (l, 66 * f + 3 * j + 1), (l, 66 * f + 3 * j + 2)]
        return s

    def build(self):
        nc, es, nl = self.nc, self.es, self.nl
        E = es.enter_context
        dr = lambda n, shp, kind: nc.dram_tensor(n, shp, F32, kind=kind).ap()
        self.xT = dr("xT", [D, TOK], "ExternalInput")
        self.wst = dr("wst", [nl, NBLK, P, BLKW], "ExternalInput")
        self.prm = dr("prm", [P, nl * NPRM], "ExternalInput")
        self.flg = dr("flg", [P, 2], "ExternalInput")
        self.thr = nc.dram_tensor("thr", [1, 2 * nl], mybir.dt.int32, kind="ExternalInput").ap()
        self.zer = dr("zer", [2 * P, CW], "ExternalInput")
        self.outT = dr("outT", [D, TOK], "ExternalOutput")
        self.cb = [nc.dram_tensor(f"cb{l}", [P, CW], F32).ap() for l in range(nl)]
        self.cg = [nc.dram_tensor(f"cg{l}", [2 * P, CW], F32).ap() for l in range(nl)]

        sb = lambda n, shp, dt: E(nc.sbuf_tensor(n, shp, dt))
        self.xf = sb("xf", [P, KC, TOK], F32)
        self.xb = sb("xb", [P, KC, TOK], BF16)
        self.stg = sb("stg", [P, NS, BLKW], F32)
        self.wr = sb("wr", [P, NB, BLKW], BF16)
        self.hb = sb("hb", [P, 2 * G, NT], BF16)
        self.ua = sb("ua", [P, KC, NT], BF16)
        self.ub = sb("ub", [P, KC, NT], BF16)
        self.mg = sb("mg", [P, KC, NT], BF16)
        self.scr = sb("scr", [P, NSCR, SCW], F32)
        self.xcb = sb("xcb", [P, 2, NT], BF16)
        self.prs = sb("prs", [P, nl * NPRM], F32)
        self.der = sb("der", [P, nl * NDER], F32)
        self.st = sb("st", [P, CW], F32)
        self.ci = sb("ci", [P, CW], F32)
        self.fl = sb("fl", [P, 2], F32)
        self.ones = sb("ones", [P, P], F32)
        self.cst = sb("cst", [P, 4], F32)
        self.ps = [E(nc.psum_tensor(f"ps{i}", [P, NT], F32)) for i in range(8)]

        sem = lambda n: E(nc.semaphore(n))
        self.PE = Eng("pe", nc.tensor, sem("s_pe"), inorder=True)
        self.ACT = Eng("act", nc.scalar, sem("s_act"))
        self.DVE = Eng("dve", nc.vector, sem("s_dve"))
        self.SP = Eng("sp", nc.sync, None)
        self.xf_b = [[Buf(f"xf{c}_{t}") for t in range(NTL)] for c in range(KC)]
        self.xb_b = [Buf(f"xb{t}") for t in range(NTL)]
        self.stg_b = [Buf(f"stg{i}") for i in range(NS)]
        self.stg_d = [self.new_dsem(f"d_stg{i}") for i in range(NS)]
        self.wr_b = [Buf(f"wr{i}") for i in range(NB)]
        self.hb_b = [Buf(f"hb{i}") for i in range(2 * G)]
        self.ua_b = [Buf(f"ua{i}") for i in range(KC)]
        self.ub_b = [Buf(f"ub{i}") for i in range(KC)]
        self.mg_b = [Buf(f"mg{i}") for i in range(KC)]
        self.scr_b = [Buf(f"scr{i}") for i in range(NSCR)]
        self.xcb_b = [Buf("xcb0"), Buf("xcb1")]
        self.psb = [Buf(f"psb{i}") for i in range(8)]
        self.prs_b = Buf("prs")
        self.der_b = Buf("der")
        self.st_b = Buf("st")
        self.ci_b = Buf("ci")
        self.d_ci = self.new_dsem("d_ci")
        self.d_co = self.new_dsem("d_co")
        self.d_z = self.new_dsem("d_z")
        self.s_cc = E(nc.semaphore("s_cc"))
        self.rp = E(nc.gpsimd.register("rp"))
        self.rs = E(nc.sync.register("rs"))
        self.fl_b = Buf("fl")
        self.ones_b = Buf("ones")
        self.cst_b = Buf("cst")
        self.d_misc = self.new_dsem("d_misc")
        self.d_x = self.new_dsem("d_x")
        self.d_st = self.new_dsem("d_st")
        self.d_out = self.new_dsem("d_out")
        self._ra = self._rb = self._r8 = 0

        self.seq = []
        for l in range(nl):
            self.seq += self.layer_seq(l)
        self.dptr = self.cptr = self.gptr = self.rel = 0

        E(nc.allow_low_precision("bf16 matmul operands, fp32 accumulation"))
        block = E(nc.Block())

        @block.sync
        def _(sync):
            self.body()
        es.close()
        return nc

    def S(self, i, a=0, b=NT):
        return self.scr[:, i, a:b]

    def body(self):
        nc, nl = self.nc, self.nl
        allx = [b for row in self.xf_b for b in row]
        for l in range(nl):
            self.dma(self.cg[l], self.zer, [], [], self.d_z)
            self.dma(self.cb[l], self.zer[0:P, :], [], [], self.d_z)
        nc.sync.wait_ge(self.d_z.sem, self.d_z.cnt)
        nc.gpsimd.wait_ge(self.d_z.sem, self.d_z.cnt)
        self.dma(self.prs[:], self.prm, [], [self.prs_b], self.d_misc)
        self.dma(self.fl[:], self.flg, [], [self.fl_b], self.d_misc)
        xTv = self.xT.rearrange("(c p) t -> p c t", p=P)
        for c in range(KC):
            self.dma(self.xf[:, c, :], xTv[:, c, :], [], self.xf_b[c], self.d_x)
        for b in [self.prs_b, self.fl_b]:
            b.w = (self.d_misc, self.d_misc.cnt)
        for b in allx:
            b.w = (self.d_x, self.d_x.cnt)
        self.emit(self.DVE, lambda: nc.vector.memset(self.ones[:], 1.0 / D), [], [self.ones_b])
        self.emit(self.DVE, lambda: nc.vector.memset(self.cst[:, 0:1], EPS_P), [], [self.cst_b])
        self.emit(self.DVE, lambda: nc.vector.memset(self.cst[:, 1:2], 1.0), [], [self.cst_b])
        self.pump()
        for t in range(NTL):
            for c in range(KC):
                self.act(self.xb[:, c, t * NT:(t + 1) * NT], self.xf[:, c, t * NT:(t + 1) * NT], AF.Copy,
                         [self.xf_b[c][t]], [self.xb_b[t]])
        for l in range(nl):
            pb, db = l * NPRM, l * NDER
            tmp = self.S(0, 0, 8)
            self.act(tmp, self.prs[:, pb + O_AP:pb + O_AP + 8], AF.Exp, [self.prs_b], [self.scr_b[0]], scale=-1.0)
            self.act(tmp, tmp, AF.Ln, [self.scr_b[0], self.cst_b], [self.scr_b[0]], bias=self.cst[:, 1:2], scale=1.0)
            self.tsm(self.der[:, db:db + 8], tmp, -4.0, [self.scr_b[0]], [self.der_b])
            self.tsm(self.der[:, db + 8:db + 16], tmp, -8.0, [self.scr_b[0]], [self.der_b])
            self.tsm(self.der[:, db + 16:db + 32], self.prs[:, pb + O_RGB:pb + O_RGB + 16], 0.5,
                     [self.prs_b], [self.der_b])
            self.tsm(self.der[:, db + 32:db + 48], self.prs[:, pb + O_BM:pb + O_BM + 16], 0.5,
                     [self.prs_b], [self.der_b])
        for l in range(nl):
            self.ffn(l, 0)
            self._wait(self.SP, self._deps([], [self.ci_b]))
            nc.sync.reg_load(self.rs, self.thr[0:1, nl + l:nl + l + 1])
            nc.sync.wait_ge(self.s_cc, self.rs)
            self.dma(self.ci[:], self.cg[l][0:P, :], [], [self.ci_b], self.d_ci)
            self.tsm(self.st[:], self.ci[:], self.fl[:, 1:2], [self.ci_b, self.fl_b], [self.st_b])
            for t in range(NTL):
                self.mixer_tile(l, t)
            self.dma(self.cb[l], self.st[:], [self.st_b], [], self.d_co)
            nc.gpsimd.reg_load(self.rp, self.thr[0:1, l:l + 1])
            nc.gpsimd.wait_ge(self.d_co.sem, self.rp)
            nc.gpsimd.collective_compute("AllGather", ALU.bypass,
                                         replica_groups=[[2 * g, 2 * g + 1] for g in range(self.ncores // 2)],
                                         ins=[self.cb[l]], outs=[self.cg[l]]).then_inc(self.s_cc, 1)
            self.ffn(l, 1)
        oTv = self.outT.rearrange("(c p) t -> p c t", p=P)
        for c in range(KC):
            self.dma(oTv[:, c, :], self.xf[:, c, :], self.xf_b[c], [], self.d_out)
        nc.sync.wait_ge(self.d_out.sem, self.d_out.cnt)
        nc.gpsimd.wait_ge(self.s_cc, nl)

    def ffn(self, l, f):
        nc = self.nc
        groups = [list(range(g0, min(g0 + G, FC))) for g0 in range(0, FC, G)]
        units = [(gi, t) for gi in range(len(groups)) for t in range(NTL)]
        blks = {}
        hpar = {}

        def fetch(gi):
            for j in groups[gi]:
                base = 66 * f + 3 * j
                blks[j] = [self.wget(l, base), self.wget(l, base + 1), self.wget(l, base + 2)]

        def GU(ui):
            gi, t = units[ui]
            if t == 0:
                fetch(gi)
            tsl = slice(t * NT, (t + 1) * NT)
            par = ui % 2
            hpar[ui] = par
            for q, j in enumerate(groups[gi]):
                (wg, wg_b), (wu, wu_b), _ = blks[j]
                pg = self.bankA()
                self.mm(pg, [(wg[:, kc * P:(kc + 1) * P], self.xb[:, kc, tsl]) for kc in range(KC)],
                        [wg_b, self.xb_b[t]])
                pu = self.bankA()
                self.mm(pu, [(wu[:, kc * P:(kc + 1) * P], self.xb[:, kc, tsl]) for kc in range(KC)],
                        [wu_b, self.xb_b[t]])
                si = 10 + (self._sgp % 2)
                self._sgp += 1
                self.act(self.S(si), self.ps[pg][:], AF.Silu, [self.psb[pg]], [self.scr_b[si]])
                hs = par * G + q
                self.stt(self.hb[:, hs, :], self.ps[pu][:], RS, self.S(si), ALU.mult, ALU.mult,
                         [self.psb[pu], self.scr_b[si]], [self.hb_b[hs]])
            self.pump()

        def OUT(ui):
            gi, t = units[ui]
            tsl = slice(t * NT, (t + 1) * NT)
            par = hpar[ui]
            grp = groups[gi]
            for m in range(KC):
                po = self.bankB()
                pairs, rd = [], []
                for q, j in enumerate(grp):
                    w2, w2_b = blks[j][2]
                    pairs.append((w2[:, m * P:(m + 1) * P], self.hb[:, par * G + q, :]))
                    rd += [w2_b, self.hb_b[par * G + q]]
                self.mm(po, pairs, rd)
                self.tt(self.xf[:, m, tsl], self.xf[:, m, tsl], self.ps[po][:], ALU.add,
                        [self.psb[po], self.xf_b[m][t]], [self.xf_b[m][t]])
            if t == NTL - 1:
                self.release(3 * len(grp))
            if gi == len(groups) - 1:
                self.layernorm(l, 2 * f, t)

        self._sgp = 0
        GU(0)
        for ui in range(len(units)):
            if ui + 1 < len(units):
                GU(ui + 1)
            OUT(ui)

    def layernorm(self, l, idx, t):
        nc = self.nc
        tsl = slice(t * NT, (t + 1) * NT)
        pb = l * NPRM
        xrow = [self.xf_b[c][t] for c in range(KC)]
        s1, s2, s3, mean, var = 4, 5, 6, 7, 8
        self.emit(self.DVE, lambda: nc.vector.tensor_reduce(
            out=self.S(s1), in_=self.xf[:, :, tsl].rearrange("p c t -> p t c"), axis=AX.X, op=ALU.add),
            xrow, [self.scr_b[s1]])
        for hh in range(2):
            self.act(self.scr[:, 0:4, 0:NT], self.xf[:, 4 * hh:4 * hh + 4, tsl], AF.Square,
                     xrow[4 * hh:4 * hh + 4], self.scr_b[0:4])
            dst = s2 if hh == 0 else s3
            self.emit(self.DVE, lambda: nc.vector.tensor_reduce(
                out=self.S(dst), in_=self.scr[:, 0:4, 0:NT].rearrange("p c t -> p t c"), axis=AX.X, op=ALU.add),
                self.scr_b[0:4], [self.scr_b[dst]])
        self.tt(self.S(s2), self.S(s2), self.S(s3), ALU.add, [self.scr_b[s2], self.scr_b[s3]], [self.scr_b[s2]])
        pm = self.bankA()
        self.mm(pm, [(self.ones[:], self.S(s1))], [self.ones_b, self.scr_b[s1]])
        pe2 = self.bankA()
        self.mm(pe2, [(self.ones[:], self.S(s2))], [self.ones_b, self.scr_b[s2]])
        self.act(self.S(mean), self.ps[pm][:], AF.Copy, [self.psb[pm]], [self.scr_b[mean]])
        self.tt(self.S(var), self.S(mean), self.S(mean), ALU.mult, [self.scr_b[mean]], [self.scr_b[var]])
        self.tt(self.S(var), self.ps[pe2][:], self.S(var), ALU.subtract, [self.psb[pe2], self.scr_b[var]],
                [self.scr_b[var]])
        self.act(self.S(var), self.S(var), AF.Sqrt, [self.scr_b[var], self.cst_b], [self.scr_b[var]],
                 bias=self.cst[:, 0:1], scale=1.0)
        self.emit(self.DVE, lambda: nc.vector.reciprocal(out=self.S(var), in_=self.S(var)),
                  [self.scr_b[var]], [self.scr_b[var]])
        for c in range(KC):
            xs = self.xf[:, c, tsl]
            self.tt(xs, xs, self.S(mean), ALU.subtract, [xrow[c], self.scr_b[mean]], [xrow[c]])
            self.tt(xs, xs, self.S(var), ALU.mult, [xrow[c], self.scr_b[var]], [xrow[c]])
            gcol = self.prs[:, pb + O_LNG + idx * 8 + c:pb + O_LNG + idx * 8 + c + 1]
            bcol = self.prs[:, pb + O_LNB + idx * 8 + c:pb + O_LNB + idx * 8 + c + 1]
            self.act(self.xb[:, c, tsl], xs, AF.Identity, [xrow[c], self.prs_b], [self.xb_b[t]], bias=bcol, scale=gcol)
            self.act(xs, xs, AF.Identity, [xrow[c], self.prs_b], [xrow[c]], bias=bcol, scale=gcol)

    def mixer_tile(self, l, t):
        nc = self.nc
        tsl = slice(t * NT, (t + 1) * NT)
        pb, db = l * NPRM, l * NDER
        prs, der, st = self.prs, self.der, self.st
        xbt = self.xb_b[t]
        T1, CV, XR, XC0, YS0, U, GI, GR, A, H = 0, 1, 2, 3, 5, 7, 8, 9, 10, 11
        sb = self.scr_b

        def proj(oc_blk):
            w, w_b = self.wget(l, oc_blk)
            bk = self.bank8()
            self.mm(bk, [(w[:, kc * P:(kc + 1) * P], self.xb[:, kc, tsl]) for kc in range(KC)], [w_b, xbt])
            self.release(1)
            return bk

        for h in range(4):
            for jj in range(2):
                j = 2 * h + jj
                XC, YS = XC0 + jj, YS0 + jj
                bv = proj(132 + 16 + j)
                self.act(self.S(T1), self.ps[bv][:], AF.Copy, [self.psb[bv]], [sb[T1]])
                bc = proj(132 + 8 + j)
                self.cp(self.scr[:, CV, 0:2], st[:, 8 + 2 * j:10 + 2 * j], [self.st_b], [sb[CV]])
                self.tt(self.scr[:, CV, 2:2 + NT], self.ps[bc][:], self.S(T1), ALU.mult,
                        [self.psb[bc], sb[T1]], [sb[CV]])
                wcol = lambda k: prs[:, pb + O_SCW + k * 8 + j:pb + O_SCW + k * 8 + j + 1]
                self.tsm(self.S(T1), self.scr[:, CV, 0:NT], wcol(0), [sb[CV], self.prs_b], [sb[T1]])
                for k in (1, 2):
                    self.stt(self.S(T1), self.scr[:, CV, k:k + NT], wcol(k), self.S(T1), ALU.mult, ALU.add,
                             [sb[CV], sb[T1], self.prs_b], [sb[T1]])
                self.cp(st[:, 8 + 2 * j:10 + 2 * j], self.scr[:, CV, NT:NT + 2], [sb[CV]], [self.st_b])
                bb = proj(132 + j)
                self.tt(self.ua[:, j, :], self.S(T1), self.ps[bb][:], ALU.mult, [sb[T1], self.psb[bb]],
                        [self.ua_b[j]])
                bx = proj(132 + 24 + j)
                self.cp(self.scr[:, XR, 0:3], st[:, 24 + 3 * j:27 + 3 * j], [self.st_b], [sb[XR]])
                self.act(self.scr[:, XR, 3:3 + NT], self.ps[bx][:], AF.Copy, [self.psb[bx]], [sb[XR]])
                rcol = lambda k: prs[:, pb + O_RCW + k * 8 + j:pb + O_RCW + k * 8 + j + 1]
                self.ts(self.S(XC), self.scr[:, XR, 0:NT], rcol(0), prs[:, pb + O_RCB + j:pb + O_RCB + j + 1],
                        ALU.mult, ALU.add, [sb[XR], self.prs_b], [sb[XC]])
                for k in (1, 2, 3):
                    self.stt(self.S(XC), self.scr[:, XR, k:k + NT], rcol(k), self.S(XC), ALU.mult, ALU.add,
                             [sb[XR], sb[XC], self.prs_b], [sb[XC]])
                self.cp(st[:, 24 + 3 * j:27 + 3 * j], self.scr[:, XR, NT:NT + 3], [sb[XR]], [self.st_b])
                self.act(self.xcb[:, jj, :], self.S(XC), AF.Copy, [sb[XC]], [self.xcb_b[jj]])
                by = proj(132 + 32 + j)
                self.act(self.S(YS), self.ps[by][:], AF.Copy, [self.psb[by]], [sb[YS]])
                self.act(self.S(U), self.ps[by][:], AF.Square, [self.psb[by]], [sb[U]])
                self.ts(self.S(U), self.S(U), 0.044715, 1.0, ALU.mult, ALU.add, [sb[U]], [sb[U]])
                self.tt(self.S(U), self.S(U), self.S(YS), ALU.mult, [sb[U], sb[YS]], [sb[U]])
                self.act(self.S(U), self.S(U), AF.Tanh, [sb[U]], [sb[U]], scale=0.7978845608028654)
                self.stt(self.S(YS), self.S(U), 1.0, self.S(YS), ALU.add, ALU.mult, [sb[U], sb[YS]], [sb[YS]])
            rg, rg_b = self.wget(l, 188 + h)
            for jo in range(2):
                j = 2 * h + jo
                XC, YS = XC0 + jo, YS0 + jo
                pgi = self.bank8()
                self.mm(pgi, [(rg[:, (0 * 2 + ji) * 256 + jo * P:(0 * 2 + ji) * 256 + (jo + 1) * P],
                               self.xcb[:, ji, :]) for ji in range(2)], [rg_b] + self.xcb_b)
                pgr = self.bank8()
                self.mm(pgr, [(rg[:, (1 * 2 + ji) * 256 + jo * P:(1 * 2 + ji) * 256 + (jo + 1) * P],
                               self.xcb[:, ji, :]) for ji in range(2)], [rg_b] + self.xcb_b)
                self.act(self.S(GI), self.ps[pgi][:], AF.Tanh, [self.psb[pgi], self.der_b], [sb[GI]],
                         bias=der[:, db + 16 + j:db + 17 + j], scale=0.5)
                self.act(self.S(GR), self.ps[pgr][:], AF.Tanh, [self.psb[pgr], self.der_b], [sb[GR]],
                         bias=der[:, db + 24 + j:db + 25 + j], scale=0.5)
                self.act(self.S(A), self.S(GR), AF.Exp, [sb[GR], self.der_b], [sb[A]],
                         bias=der[:, db + j:db + j + 1], scale=der[:, db + j:db + j + 1])
                self.act(self.S(GR), self.S(GR), AF.Exp, [sb[GR], self.der_b], [sb[GR]],
                         bias=der[:, db + 8 + j:db + 9 + j], scale=der[:, db + 8 + j:db + 9 + j])
                self.act(self.S(GR), self.S(GR), AF.Sqrt, [sb[GR], self.cst_b], [sb[GR]],
                         bias=self.cst[:, 1:2], scale=-1.0)
                if t == 0:
                    self.ts(self.scr[:, GR, 0:1], self.scr[:, GR, 0:1], self.fl[:, 1:2], self.fl[:, 0:1],
                            ALU.mult, ALU.add, [sb[GR], self.fl_b], [sb[GR]])
                self.stt(self.S(GI), self.S(GI), 1.0, self.S(XC), ALU.add, ALU.mult, [sb[GI], sb[XC]], [sb[GI]])
                self.stt(self.S(GI), self.S(GI), 0.5, self.S(GR), ALU.mult, ALU.mult, [sb[GI], sb[GR]], [sb[GI]])
                self.emit(self.DVE, lambda: nc.vector.tensor_tensor_scan(
                    out=self.S(H), data0=self.S(A), data1=self.S(GI), initial=st[:, j:j + 1],
                    op0=ALU.mult, op1=ALU.add), [sb[A], sb[GI], self.st_b], [sb[H]])
                self.cp(st[:, j:j + 1], self.scr[:, H, NT - 1:NT], [sb[H]], [self.st_b])
                self.stt(self.ub[:, j, :], self.S(H), 0.5, self.S(YS), ALU.mult, ALU.mult, [sb[H], sb[YS]],
                         [self.ub_b[j]])
            self.release(1)
        for m in range(KC):
            pga = proj(132 + 40 + m)
            pgb = proj(132 + 48 + m)
            wa, wa_b = self.wget(l, 192 + m)
            pya = self.bank8()
            self.mm(pya, [(wa[:, kc * P:(kc + 1) * P], self.ua[:, kc, :]) for kc in range(KC)], [wa_b] + self.ua_b)
            self.release(1)
            wb, wb_b = self.wget(l, 200 + m)
            pyb = self.bank8()
            self.mm(pyb, [(wb[:, kc * P:(kc + 1) * P], self.ub[:, kc, :]) for kc in range(KC)], [wb_b] + self.ub_b)
            self.release(1)
            s0 = (m % 2) * 2
            self.act(self.S(s0), self.ps[pga][:], AF.Tanh, [self.psb[pga], self.der_b], [sb[s0]],
                     bias=der[:, db + 32 + m:db + 33 + m], scale=0.5)
            self.act(self.S(s0 + 1), self.ps[pgb][:], AF.Tanh, [self.psb[pgb], self.der_b], [sb[s0 + 1]],
                     bias=der[:, db + 40 + m:db + 41 + m], scale=0.5)
            self.stt(self.S(s0), self.S(s0), 1.0, self.ps[pya][:], ALU.add, ALU.mult, [sb[s0], self.psb[pya]], [sb[s0]])
            self.stt(self.S(s0 + 1), self.S(s0 + 1), 1.0, self.ps[pyb][:], ALU.add, ALU.mult,
                     [sb[s0 + 1], self.psb[pyb]], [sb[s0 + 1]])
            self.tt(self.mg[:, m, :], self.S(s0), self.S(s0 + 1), ALU.add, [sb[s0], sb[s0 + 1]], [self.mg_b[m]])
        for m in range(KC):
            wo, wo_b = self.wget(l, 208 + m)
            po = self.bank8()
            self.mm(po, [(wo[:, kc * P:(kc + 1) * P], self.mg[:, kc, :]) for kc in range(KC)], [wo_b] + self.mg_b)
            self.release(1)
            self.stt(self.xf[:, m, tsl], self.ps[po][:], RS, self.xf[:, m, tsl], ALU.mult, ALU.add,
                     [self.psb[po], self.xf_b[m][t]], [self.xf_b[m][t]])
        self.layernorm(l, 1, t)


def _kblocks(w, cols):
    n = w.shape[1] // cols
    return w.reshape(KC, P, n, cols).transpose(2, 1, 0, 3).reshape(n, P, KC * cols)


def _layer_stream(i, l):
    out = np.empty((NBLK, P, BLKW), np.float32)
    for f in range(2):
        w1 = i["ffn_w1"][l, f]
        w2 = i["ffn_w2"][l, f]
        gb = _kblocks(w1[:, :DFF], P)
        ub = _kblocks(w1[:, DFF:], P)
        w2b = w2.reshape(FC, P, D)
        base = 66 * f
        out[base + 0:base + 66:3] = gb
        out[base + 1:base + 66:3] = ub
        out[base + 2:base + 66:3] = w2b
    out[132:188] = _kblocks(i["w_in"][l], P)
    rg = i["rg_w"][l]
    for h in range(4):
        blk = rg[:, h].reshape(2, 2, P, 256).transpose(2, 0, 1, 3).reshape(P, 1024)
        out[188 + h] = blk
    out[192:200] = _kblocks(i["w_out_conv"][l], P)
    out[200:208] = _kblocks(i["w_out_rnn"][l], P)
    out[208:216] = _kblocks(i["w_o"][l], P)
    return out


def _layer_params(i, l):
    pr = np.empty((P, NPRM), np.float32)
    fm = lambda v: np.asarray(v, np.float32).reshape(-1, P).T
    pr[:, O_LNG:O_LNG + 24] = fm(i["ln_g"][l].reshape(-1))
    pr[:, O_LNB:O_LNB + 24] = fm(i["ln_b"][l].reshape(-1))
    pr[:, O_BM:O_BM + 16] = fm(i["b_merge"][l])
    pr[:, O_SCW:O_SCW + 24] = fm(i["sc_w"][l].reshape(-1))
    pr[:, O_RCW:O_RCW + 32] = fm(i["rc_w"][l].reshape(-1))
    pr[:, O_RCB:O_RCB + 8] = fm(i["rc_b"][l])
    pr[:, O_RGB:O_RGB + 16] = fm(i["rg_b"][l].reshape(-1))
    pr[:, O_AP:O_AP + 8] = fm(i["a_param"][l])
    return pr


_PROG_CACHE = {}


def _get_prog(nl, ncores=8):
    key = (nl, ncores)
    if key not in _PROG_CACHE:
        _PROG_CACHE[key] = Prog(nl, ncores).build()
    return _PROG_CACHE[key]


def make_in_maps(i, nl=DEPTH, ncores=8):
    x = i["x"].astype(np.float32, copy=False)
    wst = np.stack([_layer_stream(i, l) for l in range(nl)], 0)
    prm = np.concatenate([_layer_params(i, l) for l in range(nl)], 1)
    zer = np.zeros((2 * P, CW), np.float32)
    in_maps = []
    for c in range(ncores):
        b, hf = c // 2, c % 2
        f = 1 - hf
        flg = np.zeros((P, 2), np.float32)
        flg[:, 0] = f
        flg[:, 1] = 1 - f
        thr = np.array([[16 * (l + f) for l in range(nl)] + [l + (1 - f) for l in range(nl)]], np.int32)
        in_maps.append({"xT": np.ascontiguousarray(x[b, hf * TOK:(hf + 1) * TOK, :].T), "wst": wst,
                        "prm": prm, "flg": flg, "thr": thr, "zer": zer})
    return in_maps


def kernel(**inputs):
    i = {k: np.asarray(v) for k, v in inputs.items()}
    x = i["x"]
    ncores = 8
    nc = _get_prog(DEPTH, ncores)
    in_maps = make_in_maps(i, DEPTH, ncores)
    res = run_bass_kernel_spmd(nc, in_maps, core_ids=list(range(ncores)))
    out = np.empty(x.shape, np.float32)
    for c in range(ncores):
        b, hf = c // 2, c % 2
        out[b, hf * TOK:(hf + 1) * TOK, :] = res.results[c]["outT"].T
    return out
```
